# Optimizing a Trainium2 kernel written in Bass

```python
import math
import jax, jax.numpy as jnp
from jax import lax
import numpy as np

D_MODEL = 4096
BATCH = 2
SEQ = 4096
DEPTH = 2

CHUNK = 64
N_MIXERS = 2
N_A = (DEPTH + N_MIXERS - 1) // N_MIXERS
N_B = DEPTH // N_MIXERS
EPS = 1e-6
N_MOD = 6
SGU_BLOCK = 128
SGU_HIDDEN = 3 * D_MODEL
SGU_GROUPS = 16
SGU_GROUP_DIM = SGU_HIDDEN // SGU_GROUPS
MLA_HEADS = 64
Q_LORA = 1536
KV_LORA = 512
QK_NOPE = 128
QK_ROPE = 64
V_DIM = 128
ROPE_THETA = 10000.0
Q_BLOCK = 128
D_FF = 10752
CONV_W = 3

kernel_name = 'hybrid_gmlp_mla_convffn_adaln'


def _rmsnorm(x, g=None):
    xf = x.astype(jnp.float32)
    y = xf * lax.rsqrt(jnp.mean(xf * xf, axis=-1, keepdims=True) + EPS)
    if g is not None:
        y = y * g.astype(jnp.float32)
    return y.astype(x.dtype)


def _layernorm(x, g, b):
    xf = x.astype(jnp.float32)
    mu = jnp.mean(xf, axis=-1, keepdims=True)
    var = jnp.mean(jnp.square(xf - mu), axis=-1, keepdims=True)
    y = (xf - mu) * lax.rsqrt(var + EPS) * g.astype(jnp.float32) + b.astype(jnp.float32)
    return y.astype(x.dtype)


def _modulate(h, shift, scale):
    return h * (1 + scale[:, None, :]) + shift[:, None, :]


def _rope_cos_sin(positions):
    inv_freq = ROPE_THETA ** (-jnp.arange(0, QK_ROPE, 2, dtype=jnp.float32) / QK_ROPE)
    ang = positions.astype(jnp.float32)[..., None] * inv_freq
    return jnp.cos(ang), jnp.sin(ang)


def _apply_rope(x, cos, sin):
    x1, x2 = jnp.split(x.astype(jnp.float32), 2, axis=-1)
    return jnp.concatenate([x1 * cos - x2 * sin, x1 * sin + x2 * cos], axis=-1).astype(x.dtype)


def _sgu_mixer(h, w_in, b_in, ln_g, ln_b, w_s, b_s, w_out):
    bsz, seq, _ = h.shape
    z = jax.nn.gelu(h @ w_in + b_in, approximate=False)
    u, v = jnp.split(z, 2, axis=-1)
    v = _layernorm(v, ln_g, ln_b)
    t = jnp.arange(SGU_BLOCK)
    mask = (t[None, :] // CHUNK) <= (t[:, None] // CHUNK)
    w = jnp.where(mask[None], w_s, jnp.zeros_like(w_s))
    vb = v.reshape(bsz, seq // SGU_BLOCK, SGU_BLOCK, SGU_GROUPS, SGU_GROUP_DIM)
    mixed = jnp.einsum('gts,bnsgc->bntgc', w, vb) + b_s.T[None, None, :, :, None]
    return (u * mixed.reshape(bsz, seq, SGU_HIDDEN)) @ w_out


def _mla(h, positions, w_in, g_q, g_kv, w_uq, w_ukv, w_o):
    bsz, seq, _ = h.shape
    proj = h @ w_in
    c_q = _rmsnorm(proj[..., :Q_LORA], g_q)
    c_kv = _rmsnorm(proj[..., Q_LORA:Q_LORA + KV_LORA], g_kv)
    k_rope = proj[..., Q_LORA + KV_LORA:]
    q = (c_q @ w_uq).reshape(bsz, seq, MLA_HEADS, QK_NOPE + QK_ROPE)
    kv = (c_kv @ w_ukv).reshape(bsz, seq, MLA_HEADS, QK_NOPE + V_DIM)
    k_nope, v = kv[..., :QK_NOPE], kv[..., QK_NOPE:]
    cos, sin = _rope_cos_sin(positions)
    q_nope = q[..., :QK_NOPE]
    q_rope = _apply_rope(q[..., QK_NOPE:], cos[:, :, None, :], sin[:, :, None, :])
    k_rope = _apply_rope(k_rope, cos, sin)
    n_blk = seq // Q_BLOCK
    qn_blocks = q_nope.reshape(bsz, n_blk, Q_BLOCK, MLA_HEADS, QK_NOPE).transpose(1, 0, 2, 3, 4)
    qr_blocks = q_rope.reshape(bsz, n_blk, Q_BLOCK, MLA_HEADS, QK_ROPE).transpose(1, 0, 2, 3, 4)
    starts = jnp.arange(n_blk, dtype=jnp.int32) * Q_BLOCK
    key_chunk = jnp.arange(seq) // CHUNK
    scale = (QK_NOPE + QK_ROPE) ** -0.5

    def attend(args):
        qn, qr, start = args
        s = (jnp.einsum('bqhd,bkhd->bhqk', qn, k_nope)
             + jnp.einsum('bqhr,bkr->bhqk', qr, k_rope)).astype(jnp.float32) * scale
        q_chunk = (start + jnp.arange(Q_BLOCK)) // CHUNK
        allowed = key_chunk[None, :] <= q_chunk[:, None]
        s = jnp.where(allowed[None, None], s, -jnp.inf)
        p = jax.nn.softmax(s, axis=-1).astype(v.dtype)
        return jnp.einsum('bhqk,bkhd->bqhd', p, v)

    o = lax.map(attend, (qn_blocks, qr_blocks, starts))
    o = o.transpose(1, 0, 2, 3, 4).reshape(bsz, seq, MLA_HEADS * V_DIM)
    return o @ w_o


def _conv_ffn(h, w_up, conv_w, conv_b, w_down):
    seq = h.shape[1]
    a = h @ w_up
    ap = jnp.pad(a, ((0, 0), (CONV_W - 1, 0), (0, 0)))
    a = conv_b + sum(conv_w[k] * ap[:, k:k + seq] for k in range(CONV_W))
    g, u = jnp.split(a, 2, axis=-1)
    return (jax.nn.silu(g) * u) @ w_down


def setup_inputs(seed: int = 0) -> dict:
    key = jax.random.key(seed)
    ks = jax.random.split(key, 24)

    def nrm(k, shape, fan_in, gain=1.0):
        return jax.random.normal(k, shape, jnp.float32) * (gain * fan_in ** -0.5)

    def small(k, shape):
        return 0.01 * jax.random.normal(k, shape, jnp.float32)

    x = jax.random.normal(ks[0], (BATCH, SEQ, D_MODEL), jnp.float32)
    c = jax.random.normal(ks[1], (BATCH, D_MODEL), jnp.float32)
    offset = jax.random.randint(ks[2], (BATCH, 1), 0, 1024, dtype=jnp.int32)
    positions = (offset + jnp.arange(SEQ, dtype=jnp.int32)[None, :]).astype(jnp.int32)
    return {
        'x': x,
        'c': c,
        'positions': positions,
        'ada_w': nrm(ks[3], (DEPTH, D_MODEL, N_MOD * D_MODEL), D_MODEL, 0.5),
        'ada_b': small(ks[4], (DEPTH, N_MOD * D_MODEL)),
        'sgu_w_in': nrm(ks[5], (N_A, D_MODEL, 2 * SGU_HIDDEN), D_MODEL),
        'sgu_b_in': small(ks[6], (N_A, 2 * SGU_HIDDEN)),
        'sgu_ln_g': 1.0 + small(ks[7], (N_A, SGU_HIDDEN)),
        'sgu_ln_b': small(ks[8], (N_A, SGU_HIDDEN)),
        'sgu_w_s': nrm(ks[9], (N_A, SGU_GROUPS, SGU_BLOCK, SGU_BLOCK), SGU_BLOCK),
        'sgu_b_s': 1.0 + small(ks[10], (N_A, SGU_GROUPS, SGU_BLOCK)),
        'sgu_w_out': nrm(ks[11], (N_A, SGU_HIDDEN, D_MODEL), SGU_HIDDEN),
        'mla_w_in': nrm(ks[12], (N_B, D_MODEL, Q_LORA + KV_LORA + QK_ROPE), D_MODEL),
        'mla_g_q': 1.0 + small(ks[13], (N_B, Q_LORA)),
        'mla_g_kv': 1.0 + small(ks[14], (N_B, KV_LORA)),
        'mla_w_uq': nrm(ks[15], (N_B, Q_LORA, MLA_HEADS * (QK_NOPE + QK_ROPE)), Q_LORA),
        'mla_w_ukv': nrm(ks[16], (N_B, KV_LORA, MLA_HEADS * (QK_NOPE + V_DIM)), KV_LORA),
        'mla_w_o': nrm(ks[17], (N_B, MLA_HEADS * V_DIM, D_MODEL), MLA_HEADS * V_DIM),
        'ffn_w_up': nrm(ks[18], (DEPTH, D_MODEL, 2 * D_FF), D_MODEL),
        'ffn_conv_w': nrm(ks[19], (DEPTH, CONV_W, 2 * D_FF), CONV_W),
        'ffn_conv_b': small(ks[20], (DEPTH, 2 * D_FF)),
        'ffn_w_down': nrm(ks[21], (DEPTH, D_FF, D_MODEL), D_FF),
        'norm_g': 1.0 + small(ks[22], (D_MODEL,)),
    }


def reference(x, c, positions, ada_w, ada_b, sgu_w_in, sgu_b_in, sgu_ln_g, sgu_ln_b, sgu_w_s, sgu_b_s, sgu_w_out, mla_w_in, mla_g_q, mla_g_kv, mla_w_uq, mla_w_ukv, mla_w_o, ffn_w_up, ffn_conv_w, ffn_conv_b, ffn_w_down, norm_g):
    c_act = jax.nn.silu(c)
    for i in range(DEPTH):
        mod = c_act @ ada_w[i] + ada_b[i]
        sh_m, sc_m, g_m, sh_f, sc_f, g_f = jnp.split(mod, N_MOD, axis=-1)
        h = _modulate(_rmsnorm(x), sh_m, sc_m)
        j = i // N_MIXERS
        if i % N_MIXERS == 0:
            y = _sgu_mixer(h, sgu_w_in[j], sgu_b_in[j], sgu_ln_g[j], sgu_ln_b[j],
                           sgu_w_s[j], sgu_b_s[j], sgu_w_out[j])
        else:
            y = _mla(h, positions, mla_w_in[j], mla_g_q[j], mla_g_kv[j],
                     mla_w_uq[j], mla_w_ukv[j], mla_w_o[j])
        x = x + g_m[:, None, :] * y
        h = _modulate(_rmsnorm(x), sh_f, sc_f)
        x = x + g_f[:, None, :] * _conv_ffn(h, ffn_w_up[i], ffn_conv_w[i], ffn_conv_b[i], ffn_w_down[i])
    return _rmsnorm(x, norm_g)
```

```python
import contextlib
import numpy as np
import concourse.bass as bass
import concourse.mybir as mybir

F32 = mybir.dt.float32
BF16 = mybir.dt.bfloat16
I32 = mybir.dt.int32
AF = mybir.ActivationFunctionType
ALU = mybir.AluOpType
AX = mybir.AxisListType

ENGS = ("pe", "act", "dve", "pool", "sp")
DMA_WINDOW = 6


class Buf:
    __slots__ = ("name", "wc", "wd", "rc", "rd")

    def __init__(self, name=""):
        self.name = name
        self.wc = None
        self.wd = None
        self.rc = {}
        self.rd = {}


class Stage:
    def __init__(self, nc, name="st"):
        self.nc = nc
        self.name = name
        self.ops = {e: [] for e in ENGS}
        self.ndma = {e: 0 for e in ENGS}

    def _deps(self, reads, writes):
        dc, dd = {}, {}

        def addc(t):
            if t is not None:
                e, i = t
                if dc.get(e, -1) < i:
                    dc[e] = i

        def addd(t):
            if t is not None:
                k, v = t
                if dd.get(k, -1) < v:
                    dd[k] = v

        for b in reads:
            addc(b.wc)
            addd(b.wd)
        for b in writes:
            addc(b.wc)
            addd(b.wd)
            for e, i in b.rc.items():
                addc((e, i))
            for k, v in b.rd.items():
                addd((k, v))
        return dc, dd

    def op(self, eng, fn, reads=(), writes=()):
        dc, dd = self._deps(reads, writes)
        idx = len(self.ops[eng])
        self.ops[eng].append(dict(fn=fn, dc=dc, dd=dd, dma=None, inc=False))
        for b in reads:
            if b.rc.get(eng, -1) < idx:
                b.rc[eng] = idx
        for b in writes:
            b.wc = (eng, idx)
            b.wd = None
            b.rc = {}
            b.rd = {}
        return idx

    def dma(self, q, out, in_, reads=(), writes=(), **kw):
        dc, dd = self._deps(reads, writes)
        k = self.ndma[q]
        self.ndma[q] += 1
        slot = k % DMA_WINDOW
        val = 16 * (k // DMA_WINDOW + 1)
        key = (q, slot)
        if k >= DMA_WINDOW:
            if dd.get(key, -1) < val - 16:
                dd[key] = val - 16
        idx = len(self.ops[q])
        fn = (lambda e, out=out, in_=in_, kw=kw: e.dma_start(out=out, in_=in_, **kw))
        self.ops[q].append(dict(fn=fn, dc=dc, dd=dd, dma=(key, val), inc=False))
        for b in reads:
            if b.rd.get(key, -1) < val:
                b.rd[key] = val
        for b in writes:
            b.wc = None
            b.wd = (key, val)
            b.rc = {}
            b.rd = {}
        return (key, val)

    def emit(self, final_dma_wait=True):
        nc = self.nc
        for e in ENGS:
            for o in self.ops[e]:
                for de, di in o["dc"].items():
                    if de == "pe" and e == "pe":
                        continue
                    self.ops[de][di]["inc"] = True
        cnt = {}
        for e in ENGS:
            c = 0
            arr = []
            for o in self.ops[e]:
                if o["inc"]:
                    c += 1
                arr.append(c)
            cnt[e] = arr
        with contextlib.ExitStack() as es:
            csem = {e: es.enter_context(nc.semaphore(f"{self.name}_c_{e}")) for e in ENGS}
            dsem = {}
            for q in ENGS:
                for s in range(min(DMA_WINDOW, self.ndma[q])):
                    dsem[(q, s)] = es.enter_context(nc.semaphore(f"{self.name}_d_{q}{s}"))
            block = es.enter_context(nc.Block())
            last_dma = {}
            for q in ENGS:
                for o in self.ops[q]:
                    if o["dma"] is not None:
                        k, v = o["dma"]
                        last_dma[k] = max(last_dma.get(k, 0), v)

            def body(e, eng):
                seen = {}
                for o in self.ops[e]:
                    for de, di in o["dc"].items():
                        if de == "pe" and e == "pe":
                            continue
                        v = cnt[de][di]
                        key = ("c", de)
                        if seen.get(key, 0) < v:
                            eng.wait_ge(csem[de], v)
                            seen[key] = v
                    for k, v in o["dd"].items():
                        key = ("d", k)
                        if seen.get(key, 0) < v:
                            eng.wait_ge(dsem[k], v)
                            seen[key] = v
                    ins = o["fn"](eng)
                    if o["dma"] is not None:
                        k, v = o["dma"]
                        ins.then_inc(dsem[k], 16)
                    elif o["inc"]:
                        ins.then_inc(csem[e], 1)
                if final_dma_wait:
                    for k, v in last_dma.items():
                        if k[0] == e:
                            eng.wait_ge(dsem[k], v)

            if self.ops["pe"]:
                block.tensor(lambda eng: body("pe", eng))
            if self.ops["act"]:
                block.scalar(lambda eng: body("act", eng))
            if self.ops["dve"]:
                block.vector(lambda eng: body("dve", eng))
            if self.ops["pool"]:
                block.gpsimd(lambda eng: body("pool", eng))
            if self.ops["sp"]:
                block.sync(lambda eng: body("sp", eng))


import contextlib
import numpy as np

T = 1024
D = 4096
KC = 32
EPS = 1e-6


_UID = [0]


def sb(nc, es, name, shape, dt):
    _UID[0] += 1
    return es.enter_context(nc.sbuf_tensor(f"{name}_u{_UID[0]}", shape, dt))


def ps(nc, es, name, shape, dt=F32):
    _UID[0] += 1
    return es.enter_context(nc.psum_tensor(f"{name}_u{_UID[0]}", shape, dt))


class Ring:
    def __init__(self, nc, es, name, shape, dt, n, psum=False):
        mk = ps if psum else sb
        self.t = [mk(nc, es, f"{name}{i}", shape, dt) for i in range(n)]
        self.b = [Buf(f"{name}{i}") for i in range(n)]
        self.i = 0
        self.n = n

    def next(self):
        k = self.i % self.n
        self.i += 1
        return self.t[k], self.b[k]


class WStream:
    def __init__(self, nc, es, st, name, dram, F, nstg=2, nbf=2, cast=("pool",), q="sp"):
        self.st, self.dram, self.F, self.q = st, dram, F, q
        self.stg = Ring(nc, es, name + "_s", [128, F], F32, nstg)
        self.bf = Ring(nc, es, name + "_b", [128, F], BF16, nbf)
        self.cast = cast
        self.loaded = {}
        self.k = 0

    def load(self, blk):
        t, b = self.stg.next()
        self.st.dma(self.q, t[:], self.dram[blk], writes=[b])
        self.loaded[blk] = (t, b)

    def get(self, blk):
        if blk not in self.loaded:
            self.load(blk)
        t, b = self.loaded.pop(blk)
        o, ob = self.bf.next()
        eng = self.cast[self.k % len(self.cast)]
        self.k += 1
        F = self.F
        if eng == "act":
            self.st.op("act", lambda e: e.activation(out=o[:], in_=t[:], func=AF.Copy), reads=[b], writes=[ob])
        else:
            self.st.op(eng, lambda e: e.tensor_copy(out=o[:], in_=t[:]), reads=[b], writes=[ob])
        return o, ob


def stage_norm(nc, name, xT, outT, scale_cols, shift_cols, out_dt, add_one):
    with contextlib.ExitStack() as es:
        st = Stage(nc, name)
        xa = Ring(nc, es, "xa", [128, T], F32, 3)
        sq = Ring(nc, es, "sq", [128, T], F32, 2)
        tmp = Ring(nc, es, "tmp", [128, T], F32, 2)
        ho = Ring(nc, es, "ho", [128, T], out_dt, 2)
        ones = sb(nc, es, "ones", [128, 128], F32)
        b_ones = Buf()
        sc = sb(nc, es, "sc", [128, KC], F32)
        b_sc = Buf()
        sh = sb(nc, es, "sh", [128, KC], F32)
        b_sh = Buf()
        rstd = sb(nc, es, "rstd", [128, T], F32)
        b_rstd = Buf()
        epsc = sb(nc, es, "epsc", [128, 1], F32)
        b_eps = Buf()
        ss = ps(nc, es, "ss", [128, T])
        b_ss = Buf()
        st.op("pool", lambda e: e.memset(ones[:], 1.0), writes=[b_ones])
        st.op("pool", lambda e: e.memset(epsc[:], EPS), writes=[b_eps])
        st.dma("sp", sc[:], scale_cols, writes=[b_sc])
        if shift_cols is not None:
            st.dma("sp", sh[:], shift_cols, writes=[b_sh])
        else:
            st.op("pool", lambda e: e.memset(sh[:], 0.0), writes=[b_sh])
        if add_one:
            st.op("dve", lambda e: e.tensor_scalar(out=sc[:], in0=sc[:], scalar1=1.0, scalar2=None, op0=ALU.add),
                  reads=[b_sc], writes=[b_sc])
        for fc in range(KC):
            x_t, x_b = xa.next()
            st.dma("sp", x_t[:], xT[fc * 128:(fc + 1) * 128, :], writes=[x_b])
            s_t, s_b = sq.next()
            st.op("act", lambda e, s_t=s_t, x_t=x_t: e.activation(out=s_t[:], in_=x_t[:], func=AF.Square),
                  reads=[x_b], writes=[s_b])
            for h in range(2):
                st.op("pe", lambda e, s_t=s_t, h=h, fc=fc: e.matmul(ss[:, h * 512:(h + 1) * 512], lhsT=ones[:],
                                                                   rhs=s_t[:, h * 512:(h + 1) * 512],
                                                                   start=(fc == 0), stop=(fc == KC - 1)),
                      reads=[b_ones, s_b], writes=[b_ss])
        st.op("act", lambda e: e.activation(out=rstd[:], in_=ss[:], func=AF.Sqrt, bias=epsc[:], scale=1.0 / D),
              reads=[b_ss, b_eps], writes=[b_rstd])
        st.op("dve", lambda e: e.reciprocal(out=rstd[:], in_=rstd[:]), reads=[b_rstd], writes=[b_rstd])
        for fc in range(KC):
            x_t, x_b = xa.next()
            st.dma("sp", x_t[:], xT[fc * 128:(fc + 1) * 128, :], writes=[x_b])
            t_t, t_b = tmp.next()
            st.op("dve", lambda e, t_t=t_t, x_t=x_t: e.tensor_tensor(out=t_t[:], in0=x_t[:], in1=rstd[:], op=ALU.mult),
                  reads=[x_b, b_rstd], writes=[t_b])
            o_t, o_b = ho.next()
            st.op("act", lambda e, o_t=o_t, t_t=t_t, fc=fc: e.activation(out=o_t[:], in_=t_t[:], func=AF.Identity,
                                                                        bias=sh[:, fc:fc + 1], scale=sc[:, fc:fc + 1]),
                  reads=[t_b, b_sc, b_sh], writes=[o_b])
            st.dma("sp", outT[fc * 128:(fc + 1) * 128, :], o_t[:], reads=[o_b])
        st.emit()


def load_hT(nc, es, st, hT_dram, ncols=T, col0=0, name="hT"):
    h = sb(nc, es, name, [128, KC, ncols], BF16)
    b = Buf(name)
    src = hT_dram.rearrange("(kc p) t -> p kc t", p=128)
    for q in range(4):
        st.dma("sp", h[:, q * 8:(q + 1) * 8, col0:col0 + T], src[:, q * 8:(q + 1) * 8, :], writes=[b])
    return h, b


def stage_sgu1(nc, name, hT_dram, wv, bv_row, v_out, stat_out):
    NB, nb = 48, 256
    with contextlib.ExitStack() as es:
        st = Stage(nc, name)
        h, b_h = load_hT(nc, es, st, hT_dram)
        ws = WStream(nc, es, st, "wv", wv, KC * nb, nstg=2, nbf=2, cast=("pool", "dve"))
        bvf = Ring(nc, es, "bvf", [1, nb], F32, 2)
        bvb = Ring(nc, es, "bvb", [1, nb], BF16, 2)
        onesb = sb(nc, es, "onesb", [1, 128], BF16)
        b_on = Buf()
        st.op("pool", lambda e: e.memset(onesb[:], 1.0), writes=[b_on])
        pss = Ring(nc, es, "pv", [128, 2, nb], F32, 8, psum=True)
        vo = Ring(nc, es, "vo", [128, 8, nb], BF16, 2)
        vo_bs = [[Buf() for _ in range(8)] for _ in range(2)]
        junk = Ring(nc, es, "junk", [128, nb], BF16, 2)
        ssum = sb(nc, es, "ssum", [128, 8, NB], F32)
        ssq = sb(nc, es, "ssq", [128, 8, NB], F32)
        b_ssum, b_ssq = Buf(), Buf()
        ws.load(0)
        for blk in range(NB):
            if blk + 1 < NB:
                ws.load(blk + 1)
            w, b_w = ws.get(blk)
            w3 = w[:].rearrange("p (kc n) -> p kc n", n=nb)
            bf_t, bf_b = bvf.next()
            st.dma("sp", bf_t[:], bv_row[:, blk * nb:(blk + 1) * nb], writes=[bf_b])
            bb_t, b_bv = bvb.next()
            st.op("dve", lambda e, bb_t=bb_t, bf_t=bf_t: e.tensor_copy(out=bb_t[:], in_=bf_t[:]), reads=[bf_b], writes=[b_bv])
            v_t, _ = vo.next()
            vbs = vo_bs[blk % 2]
            last = (blk == NB - 1)
            for tp in range(4):
                p_t, p_b = pss.next()
                for j in range(2):
                    tt = tp * 2 + j
                    for kc in range(KC):
                        st.op("pe", lambda e, p_t=p_t, j=j, kc=kc, tt=tt, w3=w3: e.matmul(
                            p_t[:, j, :], lhsT=h[:, kc, tt * 128:(tt + 1) * 128], rhs=w3[:, kc, :],
                            start=(kc == 0), stop=False), reads=[b_h, b_w], writes=[p_b])
                    st.op("pe", lambda e, p_t=p_t, j=j, bb_t=bb_t: e.matmul(
                        p_t[:, j, :], lhsT=onesb[:, :], rhs=bb_t[:, :], start=False, stop=True),
                        reads=[b_on, b_bv], writes=[p_b])
                for j in range(2):
                    tt = tp * 2 + j
                    st.op("act", lambda e, p_t=p_t, j=j, tt=tt, v_t=v_t, blk=blk: e.activation(
                        out=v_t[:, tt, :], in_=p_t[:, j, :], func=AF.Gelu, accum_out=ssum[:, tt, blk:blk + 1]),
                        reads=[p_b], writes=[vbs[tt]] + ([b_ssum] if last else []))
                    j_t, j_b = junk.next()
                    st.op("dve", lambda e, j_t=j_t, v_t=v_t, tt=tt, blk=blk: e.scalar_tensor_tensor(
                        out=j_t[:], in0=v_t[:, tt, :], scalar=1.0, in1=v_t[:, tt, :], op0=ALU.mult, op1=ALU.mult,
                        accum_out=ssq[:, tt, blk:blk + 1]), reads=[vbs[tt]], writes=[j_b] + ([b_ssq] if last else []))
            st.dma("sp", v_out[:, blk * nb:(blk + 1) * nb].rearrange("(tt p) c -> p tt c", p=128), v_t[:], reads=vbs)
        so = sb(nc, es, "so", [128, 2, 8], F32)
        b_so = Buf()
        m2 = sb(nc, es, "m2", [128, 8], F32)
        epsc = sb(nc, es, "epsc", [128, 1], F32)
        b_eps = Buf()
        st.op("pool", lambda e: e.memset(epsc[:], EPS), writes=[b_eps])
        st.op("dve", lambda e: e.tensor_reduce(out=so[:, 0, :], in_=ssum[:], axis=AX.X, op=ALU.add),
              reads=[b_ssum], writes=[b_so])
        st.op("dve", lambda e: e.tensor_reduce(out=so[:, 1, :], in_=ssq[:], axis=AX.X, op=ALU.add),
              reads=[b_ssq], writes=[b_so])
        C = 12288.0
        st.op("dve", lambda e: e.tensor_scalar(out=so[:], in0=so[:], scalar1=1.0 / C, scalar2=None, op0=ALU.mult),
              reads=[b_so], writes=[b_so])
        st.op("dve", lambda e: e.tensor_tensor(out=m2[:], in0=so[:, 0, :], in1=so[:, 0, :], op=ALU.mult),
              reads=[b_so], writes=[b_so])
        st.op("dve", lambda e: e.tensor_tensor(out=so[:, 1, :], in0=so[:, 1, :], in1=m2[:], op=ALU.subtract),
              reads=[b_so], writes=[b_so])
        st.op("act", lambda e: e.activation(out=so[:, 1, :], in_=so[:, 1, :], func=AF.Sqrt, bias=epsc[:], scale=1.0),
              reads=[b_so, b_eps], writes=[b_so])
        st.op("dve", lambda e: e.reciprocal(out=so[:, 1, :], in_=so[:, 1, :]), reads=[b_so], writes=[b_so])
        st.dma("sp", stat_out, so[:], reads=[b_so])
        st.emit()


def stage_sgu2(nc, name, hT_dram, v_dram, stat_dram, wu, bu_cols, lng_bc, lnb_bc, wsT, bs_row, zT_out):
    with contextlib.ExitStack() as es:
        st = Stage(nc, name)
        h, b_h = load_hT(nc, es, st, hT_dram)
        ws = WStream(nc, es, st, "wu", wu, KC * 128, nstg=2, nbf=2, cast=("pool",))
        bu = sb(nc, es, "bu", [128, 96], F32); b_bu = Buf()
        st.dma("sp", bu[:], bu_cols, writes=[b_bu])
        stt = sb(nc, es, "stt", [128, 2, 8], F32); b_stt = Buf()
        st.dma("sp", stt[:], stat_dram, writes=[b_stt])
        wsf = sb(nc, es, "wsf", [128, 16, 128], F32); b_wsf = Buf()
        wsm = sb(nc, es, "wsm", [128, 16, 128], BF16); b_wsm = Buf()
        st.dma("sp", wsf[:], wsT.rearrange("p (g t) -> p g t", g=16), writes=[b_wsf])
        st.op("dve", lambda e: e.memset(wsf[64:128, :, 0:64], 0.0), reads=[b_wsf], writes=[b_wsf])
        st.op("dve", lambda e: e.tensor_copy(out=wsm[:], in_=wsf[:]), reads=[b_wsf], writes=[b_wsm])
        bsf = sb(nc, es, "bsf", [1, 2048], F32); b_bsf = Buf()
        bsh = sb(nc, es, "bsh", [1, 2048], BF16); b_bsh = Buf()
        bsl = sb(nc, es, "bsl", [1, 2048], BF16); b_bsl = Buf()
        onesb = sb(nc, es, "onesb", [1, 128], BF16); b_on = Buf()
        st.op("pool", lambda e: e.memset(onesb[:], 1.0), writes=[b_on])
        st.dma("sp", bsf[:], bs_row, writes=[b_bsf])
        st.op("dve", lambda e: e.tensor_copy(out=bsh[:], in_=bsf[:]), reads=[b_bsf], writes=[b_bsh])
        st.op("dve", lambda e: e.tensor_tensor(out=bsf[:], in0=bsf[:], in1=bsh[:], op=ALU.subtract),
              reads=[b_bsf, b_bsh], writes=[b_bsf])
        st.op("dve", lambda e: e.tensor_copy(out=bsl[:], in_=bsf[:]), reads=[b_bsf], writes=[b_bsl])
        vraw = Ring(nc, es, "vraw", [128, 8, 768], BF16, 2)
        vln = Ring(nc, es, "vln", [128, 8, 768], BF16, 1)
        grow = Ring(nc, es, "grow", [128, 768], F32, 1)
        brow = Ring(nc, es, "brow", [128, 768], F32, 1)
        tmp = Ring(nc, es, "tmp", [128, 768], F32, 2)
        ub = Ring(nc, es, "ub", [128, T], BF16, 2)
        zt = Ring(nc, es, "zt", [128, T], BF16, 2)
        pu = Ring(nc, es, "pu", [128, T], F32, 2, psum=True)
        pm = Ring(nc, es, "pm", [128, T], F32, 2, psum=True)
        ws.load(0)
        for g in range(16):
            vr_t, vr_b = vraw.next()
            st.dma("sp", vr_t[:], v_dram[:, g * 768:(g + 1) * 768].rearrange("(tt p) c -> p tt c", p=128), writes=[vr_b])
            g_t, g_b = grow.next()
            b_t, b_b = brow.next()
            st.dma("sp", g_t[:], lng_bc[:, g * 768:(g + 1) * 768], writes=[g_b])
            st.dma("sp", b_t[:], lnb_bc[:, g * 768:(g + 1) * 768], writes=[b_b])
            vl_t, vl_b = vln.next()
            for tt in range(8):
                t_t, t_b = tmp.next()
                st.op("dve", lambda e, t_t=t_t, vr_t=vr_t, tt=tt: e.tensor_scalar(
                    out=t_t[:], in0=vr_t[:, tt, :], scalar1=stt[:, 0, tt:tt + 1], scalar2=stt[:, 1, tt:tt + 1],
                    op0=ALU.subtract, op1=ALU.mult), reads=[vr_b, b_stt], writes=[t_b])
                st.op("dve", lambda e, t_t=t_t, g_t=g_t: e.tensor_tensor(out=t_t[:], in0=t_t[:], in1=g_t[:], op=ALU.mult),
                      reads=[t_b, g_b], writes=[t_b])
                st.op("dve", lambda e, t_t=t_t, b_t=b_t, vl_t=vl_t, tt=tt: e.tensor_tensor(
                    out=vl_t[:, tt, :], in0=t_t[:], in1=b_t[:], op=ALU.add), reads=[t_b, b_b], writes=[vl_b])
            for cl in range(6):
                cc = g * 6 + cl
                if cc + 1 < 96:
                    ws.load(cc + 1)
                w, b_w = ws.get(cc)
                w3 = w[:].rearrange("p (kc n) -> p kc n", n=128)
                pu_t, pu_b = pu.next()
                for h2 in range(2):
                    for kc in range(KC):
                        st.op("pe", lambda e, pu_t=pu_t, h2=h2, kc=kc, w3=w3: e.matmul(
                            pu_t[:, h2 * 512:(h2 + 1) * 512], lhsT=w3[:, kc, :], rhs=h[:, kc, h2 * 512:(h2 + 1) * 512],
                            start=(kc == 0), stop=(kc == KC - 1)), reads=[b_h, b_w], writes=[pu_b])
                u_t, u_b = ub.next()
                st.op("act", lambda e, u_t=u_t, pu_t=pu_t, cc=cc: e.activation(
                    out=u_t[:], in_=pu_t[:], func=AF.Gelu, bias=bu[:, cc:cc + 1], scale=1.0),
                    reads=[pu_b, b_bu], writes=[u_b])
                pm_t, pm_b = pm.next()
                for tt in range(8):
                    o = pm_t[:, tt * 128:(tt + 1) * 128]
                    st.op("pe", lambda e, o=o, vl_t=vl_t, tt=tt, cl=cl, g=g: e.matmul(
                        o, lhsT=vl_t[:, tt, cl * 128:(cl + 1) * 128], rhs=wsm[:, g, :], start=True, stop=False),
                        reads=[vl_b, b_wsm], writes=[pm_b])
                    st.op("pe", lambda e, o=o, g=g: e.matmul(o, lhsT=onesb[:, :], rhs=bsh[:, g * 128:(g + 1) * 128],
                                                             start=False, stop=False), reads=[b_on, b_bsh], writes=[pm_b])
                    st.op("pe", lambda e, o=o, g=g: e.matmul(o, lhsT=onesb[:, :], rhs=bsl[:, g * 128:(g + 1) * 128],
                                                             start=False, stop=True), reads=[b_on, b_bsl], writes=[pm_b])
                z_t, z_b = zt.next()
                st.op("dve", lambda e, z_t=z_t, pm_t=pm_t, u_t=u_t: e.tensor_tensor(
                    out=z_t[:], in0=pm_t[:], in1=u_t[:], op=ALU.mult), reads=[pm_b, u_b], writes=[z_b])
                st.dma("sp", zT_out[cc * 128:(cc + 1) * 128, :], z_t[:], reads=[z_b])
        st.emit()


def stage_outproj(nc, name, zT_dram, K, wt, gate_cols, xT_in, xT_out, P, CP, SBk):
    NSUB = CP // SBk
    assert P * CP * 128 == K and NSUB * SBk == CP
    with contextlib.ExitStack() as es:
        st = Stage(nc, name)
        z = sb(nc, es, "z", [128, CP, T], BF16); b_z = Buf()
        ws = WStream(nc, es, st, "wo", wt, SBk * 512, nstg=2, nbf=2, cast=("pool",))
        gt = sb(nc, es, "gt", [128, KC], F32); b_gt = Buf()
        st.dma("sp", gt[:], gate_cols, writes=[b_gt])
        acc = Ring(nc, es, "acc", [128, 512], F32, 8, psum=True)
        xo = Ring(nc, es, "xo", [128, 512], F32, 4)
        xn = Ring(nc, es, "xn", [128, 512], F32, 4)
        dtile = [[Buf() for _ in range(2)] for _ in range(KC)]
        zsrc = zT_dram.rearrange("(c p) t -> p c t", p=128)
        blk = 0
        ws.load(0)
        nblk = P * 8 * NSUB
        for part in range(P):
            nq = 4
            step = CP // nq if CP % nq == 0 else CP
            for q in range(0, CP, step):
                st.dma("sp", z[:, q:q + step, :], zsrc[:, part * CP + q:part * CP + q + step, :], writes=[b_z])
            for quad in range(8):
                tiles = [[acc.next() for _ in range(2)] for _ in range(4)]
                for sub in range(NSUB):
                    if blk + 1 < nblk:
                        ws.load(blk + 1)
                    w, b_w = ws.get(blk)
                    blk += 1
                    w3 = w[:].rearrange("p (c n) -> p c n", n=512)
                    for ccl in range(SBk):
                        c = sub * SBk + ccl
                        for fi in range(4):
                            for h2 in range(2):
                                a_t, a_b = tiles[fi][h2]
                                st.op("pe", lambda e, a_t=a_t, w3=w3, ccl=ccl, fi=fi, c=c, h2=h2: e.matmul(
                                    a_t[:], lhsT=w3[:, ccl, fi * 128:(fi + 1) * 128], rhs=z[:, c, h2 * 512:(h2 + 1) * 512],
                                    start=(c == 0), stop=(c == CP - 1)), reads=[b_z, b_w], writes=[a_b])
                src = xT_in if part == 0 else xT_out
                for fi in range(4):
                    fc = quad * 4 + fi
                    for h2 in range(2):
                        a_t, a_b = tiles[fi][h2]
                        o_t, o_b = xo.next()
                        st.dma("sp", o_t[:], src[fc * 128:(fc + 1) * 128, h2 * 512:(h2 + 1) * 512],
                               reads=[dtile[fc][h2]], writes=[o_b])
                        n_t, n_b = xn.next()
                        st.op("dve", lambda e, n_t=n_t, a_t=a_t, o_t=o_t, fc=fc: e.scalar_tensor_tensor(
                            out=n_t[:], in0=a_t[:], scalar=gt[:, fc:fc + 1], in1=o_t[:], op0=ALU.mult, op1=ALU.add),
                            reads=[a_b, o_b, b_gt], writes=[n_b])
                        st.dma("sp", xT_out[fc * 128:(fc + 1) * 128, h2 * 512:(h2 + 1) * 512], n_t[:],
                               reads=[n_b], writes=[dtile[fc][h2]])
        st.emit()


def stage_ffnup(nc, name, hT_dram, halo_dram, wup, cw_cols, cb_cols, zT_out):
    NU = 168
    TW = 342
    with contextlib.ExitStack() as es:
        st = Stage(nc, name)
        h, b_h = load_hT(nc, es, st, hT_dram, ncols=T + 2, col0=2)
        b_hal = Buf()
        st.dma("sp", h[:, :, 0:2], halo_dram.rearrange("(kc p) t -> p kc t", p=128), writes=[b_hal])
        ws = WStream(nc, es, st, "wup", wup, KC * 128, nstg=3, nbf=2, cast=("pool",))
        cw = sb(nc, es, "cw", [128, 3, NU], F32); b_cw = Buf()
        cb = sb(nc, es, "cb", [128, NU], F32); b_cb = Buf()
        st.dma("sp", cw[:], cw_cols.rearrange("p (k u) -> p k u", k=3), writes=[b_cw])
        st.dma("sp", cb[:], cb_cols, writes=[b_cb])
        pa = Ring(nc, es, "pa", [128, 3, 512], F32, 2, psum=True)
        A = Ring(nc, es, "A", [128, 3 * TW], F32, 2)
        tt_ = Ring(nc, es, "tc", [128, T], F32, 2)
        sg = Ring(nc, es, "sg", [128, T], F32, 2)
        zt = Ring(nc, es, "zt", [128, T], BF16, 2)
        ws.load(0)
        ws.load(1)
        s_t = s_b = None
        for u in range(NU):
            if u + 2 < NU:
                ws.load(u + 2)
            w, b_w = ws.get(u)
            w3 = w[:].rearrange("p (kc n) -> p kc n", n=128)
            p_t, p_b = pa.next()
            for i in range(3):
                for kc in range(KC):
                    st.op("pe", lambda e, p_t=p_t, i=i, kc=kc, w3=w3: e.matmul(
                        p_t[:, i, 0:TW], lhsT=w3[:, kc, :], rhs=h[:, kc, i * TW:(i + 1) * TW],
                        start=(kc == 0), stop=(kc == KC - 1)), reads=[b_h, b_hal, b_w], writes=[p_b])
            a_t, a_b = A.next()
            st.op("act", lambda e, a_t=a_t, p_t=p_t: e.activation(
                out=a_t[:].rearrange("p (i n) -> p i n", i=3), in_=p_t[:, :, 0:TW], func=AF.Copy),
                reads=[p_b], writes=[a_b])
            c_t, c_b = tt_.next()
            st.op("dve", lambda e, c_t=c_t, a_t=a_t, u=u: e.tensor_scalar(
                out=c_t[:], in0=a_t[:, 2:T + 2], scalar1=cw[:, 2, u:u + 1], scalar2=cb[:, u:u + 1],
                op0=ALU.mult, op1=ALU.add), reads=[a_b, b_cw, b_cb], writes=[c_b])
            st.op("dve", lambda e, c_t=c_t, a_t=a_t, u=u: e.scalar_tensor_tensor(
                out=c_t[:], in0=a_t[:, 1:T + 1], scalar=cw[:, 1, u:u + 1], in1=c_t[:], op0=ALU.mult, op1=ALU.add),
                reads=[a_b, b_cw, c_b], writes=[c_b])
            st.op("dve", lambda e, c_t=c_t, a_t=a_t, u=u: e.scalar_tensor_tensor(
                out=c_t[:], in0=a_t[:, 0:T], scalar=cw[:, 0, u:u + 1], in1=c_t[:], op0=ALU.mult, op1=ALU.add),
                reads=[a_b, b_cw, c_b], writes=[c_b])
            if u % 2 == 0:
                s_t, s_b = sg.next()
                st.op("act", lambda e, s_t=s_t, c_t=c_t: e.activation(out=s_t[:], in_=c_t[:], func=AF.Silu),
                      reads=[c_b], writes=[s_b])
            else:
                z_t, z_b = zt.next()
                st.op("dve", lambda e, z_t=z_t, s_t=s_t, c_t=c_t: e.tensor_tensor(
                    out=z_t[:], in0=s_t[:], in1=c_t[:], op=ALU.mult), reads=[s_b, c_b], writes=[z_b])
                j = u // 2
                st.dma("sp", zT_out[j * 128:(j + 1) * 128, :], z_t[:], reads=[z_b])
        st.emit()


def stage_rope(nc, name, pos_bc, invf, sgn, mul_bc, cos_out, sin_out):
    S = [64, T]
    with contextlib.ExitStack() as es:
        st = Stage(nc, name)
        pi_ = sb(nc, es, "pi", S, I32); b = Buf()
        yf = sb(nc, es, "yf", S, F32)
        y = sb(nc, es, "y", S, F32)
        fr = sb(nc, es, "fr", S, F32)
        m = sb(nc, es, "m", S, F32)
        ii = sb(nc, es, "ii", S, I32)
        mu = sb(nc, es, "mu", S, F32)
        iv = sb(nc, es, "iv", [64, 1], F32)
        sg = sb(nc, es, "sg", [64, 1], F32)
        st.dma("sp", pi_[:], pos_bc, writes=[b])
        st.dma("sp", iv[:], invf, writes=[b])
        st.dma("sp", sg[:], sgn, writes=[b])
        if mul_bc is not None:
            st.dma("sp", mu[:], mul_bc, writes=[b])

        def dv(fn):
            st.op("dve", fn, reads=[b], writes=[b])
        dv(lambda e: e.tensor_copy(out=yf[:], in_=pi_[:]))
        dv(lambda e: e.tensor_scalar(out=y[:], in0=yf[:], scalar1=iv[:, 0:1], scalar2=1.0 / (2 * np.pi),
                                     op0=ALU.mult, op1=ALU.mult))
        for which, off, out in (("s", 0.0, sin_out), ("c", 0.25, cos_out)):
            dv(lambda e, off=off: e.tensor_scalar(out=fr[:], in0=y[:], scalar1=off, scalar2=None, op0=ALU.add))
            dv(lambda e: e.tensor_copy(out=ii[:], in_=fr[:]))
            dv(lambda e: e.tensor_copy(out=yf[:], in_=ii[:]))
            dv(lambda e: e.tensor_tensor(out=fr[:], in0=fr[:], in1=yf[:], op=ALU.subtract))
            dv(lambda e: e.tensor_scalar(out=m[:], in0=fr[:], scalar1=0.5, scalar2=None, op0=ALU.is_gt))
            dv(lambda e: e.tensor_tensor(out=fr[:], in0=fr[:], in1=m[:], op=ALU.subtract))
            dv(lambda e: e.tensor_scalar(out=m[:], in0=fr[:], scalar1=-0.5, scalar2=None, op0=ALU.is_lt))
            dv(lambda e: e.tensor_tensor(out=fr[:], in0=fr[:], in1=m[:], op=ALU.add))
            st.op("act", lambda e: e.activation(out=m[:], in_=fr[:], func=AF.Sin, scale=6.283185), reads=[b], writes=[b])
            if which == "s":
                dv(lambda e: e.tensor_scalar(out=m[:], in0=m[:], scalar1=sg[:, 0:1], scalar2=None, op0=ALU.mult))
            if mul_bc is not None:
                dv(lambda e: e.tensor_tensor(out=m[:], in0=m[:], in1=mu[:], op=ALU.mult))
            st.dma("sp", out, m[:], reads=[b], writes=[b])
        st.emit()


def stage_mlaproj(nc, name, hT_dram, win, gq_cols, gkv_cols, cos_d, sin_d, cqg_out, ckvg_out, krope_out,
                  rq_out, rkv_out):
    with contextlib.ExitStack() as es:
        st = Stage(nc, name)
        h, b_h = load_hT(nc, es, st, hT_dram)
        ws = WStream(nc, es, st, "win", win, KC * 128, nstg=2, nbf=2, cast=("pool",))
        gq = sb(nc, es, "gq", [128, 12], F32); b_gq = Buf()
        gkv = sb(nc, es, "gkv", [128, 4], F32); b_gkv = Buf()
        st.dma("sp", gq[:], gq_cols, writes=[b_gq])
        st.dma("sp", gkv[:], gkv_cols, writes=[b_gkv])
        ones = sb(nc, es, "ones", [128, 128], F32); b_ones = Buf()
        st.op("pool", lambda e: e.memset(ones[:], 1.0), writes=[b_ones])
        epsc = sb(nc, es, "epsc", [128, 1], F32); b_eps = Buf()
        st.op("pool", lambda e: e.memset(epsc[:], EPS), writes=[b_eps])
        cs = sb(nc, es, "cs", [64, T], F32); b_cs = Buf()
        sn = sb(nc, es, "sn", [64, T], F32); b_sn = Buf()
        st.dma("sp", cs[:], cos_d, writes=[b_cs])
        st.dma("sp", sn[:], sin_d, writes=[b_sn])
        pm = Ring(nc, es, "pm", [128, T], F32, 2, psum=True)
        ssq = ps(nc, es, "ssq", [128, T]); b_ssq = Buf()
        sskv = ps(nc, es, "sskv", [128, T]); b_sskv = Buf()
        sq = Ring(nc, es, "sq", [128, T], F32, 2)
        og = Ring(nc, es, "og", [128, T], BF16, 2)
        pend = []
        ws.load(0)
        for blk in range(17):
            if blk + 1 < 17:
                ws.load(blk + 1)
            w, b_w = ws.get(blk)
            w3 = w[:].rearrange("p (kc n) -> p kc n", n=128)
            p_t, p_b = pm.next()
            for h2 in range(2):
                for kc in range(KC):
                    st.op("pe", lambda e, p_t=p_t, h2=h2, kc=kc, w3=w3: e.matmul(
                        p_t[:, h2 * 512:(h2 + 1) * 512], lhsT=w3[:, kc, :], rhs=h[:, kc, h2 * 512:(h2 + 1) * 512],
                        start=(kc == 0), stop=(kc == KC - 1)), reads=[b_h, b_w], writes=[p_b])
            for f in pend:
                f()
            pend = []
            if blk < 16:
                isq = blk < 12
                gcol = gq[:, blk:blk + 1] if isq else gkv[:, blk - 12:blk - 11]
                o_t, o_b = og.next()
                st.op("act", lambda e, o_t=o_t, p_t=p_t, gcol=gcol: e.activation(
                    out=o_t[:], in_=p_t[:], func=AF.Identity, scale=gcol), reads=[p_b, b_gq, b_gkv], writes=[o_b])
                dst = cqg_out[blk * 128:(blk + 1) * 128, :] if isq else ckvg_out[(blk - 12) * 128:(blk - 11) * 128, :]
                st.dma("sp", dst, o_t[:], reads=[o_b])
                s_tile, s_b = sq.next()
                s_t = s_tile[:]
                st.op("act", lambda e, s_t=s_t, p_t=p_t: e.activation(out=s_t, in_=p_t[:], func=AF.Square),
                      reads=[p_b], writes=[s_b])
                acc, b_acc = (ssq, b_ssq) if isq else (sskv, b_sskv)
                first = blk in (0, 12)
                lastb = blk in (11, 15)

                def red(acc=acc, b_acc=b_acc, s_t=s_t, s_b=s_b, first=first, lastb=lastb):
                    for h2 in range(2):
                        st.op("pe", lambda e, h2=h2: e.matmul(acc[:, h2 * 512:(h2 + 1) * 512], lhsT=ones[:],
                                                              rhs=s_t[:, h2 * 512:(h2 + 1) * 512], start=first, stop=lastb),
                              reads=[b_ones, s_b], writes=[b_acc])
                pend.append(red)
            else:
                xk = sb(nc, es, "xk", [64, T], F32); b_xk = Buf()
                xs = sb(nc, es, "xs", [64, T], F32); b_xs = Buf()
                st.op("act", lambda e, p_t=p_t: e.activation(out=xk[:], in_=p_t[0:64, :], func=AF.Copy), reads=[p_b], writes=[b_xk])
                st.op("dve", lambda e: e.tensor_copy(out=xs[0:32, :], in_=xk[32:64, :]), reads=[b_xk], writes=[b_xs])
                st.op("dve", lambda e: e.tensor_copy(out=xs[32:64, :], in_=xk[0:32, :]), reads=[b_xk], writes=[b_xs])
                st.op("dve", lambda e: e.tensor_tensor(out=xk[:], in0=xk[:], in1=cs[:], op=ALU.mult), reads=[b_xk, b_cs], writes=[b_xk])
                st.op("dve", lambda e: e.tensor_tensor(out=xs[:], in0=xs[:], in1=sn[:], op=ALU.mult), reads=[b_xs, b_sn], writes=[b_xs])
                kr = sb(nc, es, "kr", [64, T], BF16); b_kr = Buf()
                st.op("dve", lambda e: e.tensor_tensor(out=kr[:], in0=xk[:], in1=xs[:], op=ALU.add), reads=[b_xk, b_xs], writes=[b_kr])
                st.dma("sp", krope_out, kr[:], reads=[b_kr])
        for f in pend:
            f()
        for acc, b_acc, n, dst in ((ssq, b_ssq, 1536.0, rq_out), (sskv, b_sskv, 512.0, rkv_out)):
            r = sb(nc, es, "r", [128, T], F32); b_r = Buf()
            st.op("act", lambda e, r=r, acc=acc, n=n: e.activation(out=r[:], in_=acc[:], func=AF.Sqrt, bias=epsc[:], scale=1.0 / n),
                  reads=[b_acc, b_eps], writes=[b_r])
            st.op("dve", lambda e, r=r: e.reciprocal(out=r[:], in_=r[:]), reads=[b_r], writes=[b_r])
            st.dma("sp", dst, r[0:1, :], reads=[b_r])
        st.emit()


def stage_attn(nc, name, cqg_d, rqs_bc_d, ckvg_d, rkv_bc_d, rkvcol_d, ka_d, qb_d, cosq_d, sinq_d, wq_d, wk_d, wv_d, oT_out):
    NKT = 32
    with contextlib.ExitStack() as es:
        st = Stage(nc, name)
        cq = sb(nc, es, "cq", [128, 12, T], BF16); b_cq = Buf()
        st.dma("sp", cq[:], cqg_d.rearrange("(c p) t -> p c t", p=128), writes=[b_cq])
        ckv = sb(nc, es, "ckv", [128, 4, 4096], BF16); b_ckv = Buf()
        st.dma("sp", ckv[:], ckvg_d.rearrange("(c p) t -> p c t", p=128), writes=[b_ckv])
        ka = sb(nc, es, "ka", [128, 4096], BF16); b_ka = Buf()
        st.dma("sp", ka[:], ka_d, writes=[b_ka])
        rkv = sb(nc, es, "rkv", [128, 4096], F32); b_rkv = Buf()
        st.dma("sp", rkv[:], rkv_bc_d, writes=[b_rkv])
        rkc = sb(nc, es, "rkc", [128, 32], F32); b_rkc = Buf()
        st.dma("sp", rkc[:], rkvcol_d, writes=[b_rkc])
        rqs = sb(nc, es, "rqs", [128, T], F32); b_rqs = Buf()
        st.dma("sp", rqs[:], rqs_bc_d, writes=[b_rqs])
        cs = sb(nc, es, "cs", [64, T], F32); b_cs = Buf()
        sn = sb(nc, es, "sn", [64, T], F32); b_sn = Buf()
        st.dma("sp", cs[:], cosq_d, writes=[b_cs])
        st.dma("sp", sn[:], sinq_d, writes=[b_sn])
        ones = sb(nc, es, "ones", [128, 128], F32); b_ones = Buf()
        st.op("pool", lambda e: e.memset(ones[:], 1.0), writes=[b_ones])
        qa = [sb(nc, es, f"qa{i}", [128, T], BF16) for i in range(2)]
        b_qa = [Buf(), Buf()]
        for i in range(2):
            st.dma("sp", qa[i][64:128, :], qb_d, writes=[b_qa[i]])
        wqs = WStream(nc, es, st, "wq", wq_d, 12 * 192, nstg=1, nbf=2, cast=("pool",))
        wks = WStream(nc, es, st, "wk", wk_d, 4 * 128, nstg=2, nbf=2, cast=("pool",))
        wvs = WStream(nc, es, st, "wv", wv_d, 4 * 256, nstg=2, nbf=2, cast=("pool",))
        KT = Ring(nc, es, "KT", [128, 4096], BF16, 2)
        V2 = Ring(nc, es, "V2", [128, NKT, 256], BF16, 1)
        QT = Ring(nc, es, "QT", [128, T], BF16, 2)
        xq = Ring(nc, es, "xq", [64, 512], F32, 2)
        xs = Ring(nc, es, "xs", [64, 512], F32, 2)
        pt = Ring(nc, es, "pt", [128, 512], BF16, 3)
        dA = Ring(nc, es, "dA", [128, 512], F32, 1)
        dB = Ring(nc, es, "dB", [128, 512], F32, 1)
        rb = Ring(nc, es, "rb", [128, 512], F32, 1)
        ot = Ring(nc, es, "ot", [128, 512], BF16, 2)
        pg = Ring(nc, es, "pg", [128, 512], F32, 2, psum=True)
        pS = Ring(nc, es, "pS", [128, 512], F32, 2, psum=True)
        pO = Ring(nc, es, "pO", [128, 512], F32, 2, psum=True)
        pD = Ring(nc, es, "pD", [128, 512], F32, 1, psum=True)
        v_t = v_b = None
        for hd in range(64):
            hl = hd % 2
            if hl == 0:
                wv, b_wv = wvs.get(hd // 2)
                if hd // 2 + 1 < 32:
                    wvs.load(hd // 2 + 1)
                wv3 = wv[:].rearrange("p (c n) -> p c n", n=256)
                v_t, v_b = V2.next()
                for kp in range(NKT // 2):
                    g_t, g_b = pg.next()
                    for j in range(2):
                        kt = kp * 2 + j
                        for c in range(4):
                            st.op("pe", lambda e, g_t=g_t, j=j, kt=kt, c=c, wv3=wv3: e.matmul(
                                g_t[:, j * 256:(j + 1) * 256], lhsT=ckv[:, c, kt * 128:(kt + 1) * 128], rhs=wv3[:, c, :],
                                start=(c == 0), stop=(c == 3)), reads=[b_ckv, b_wv], writes=[g_b])
                    for j in range(2):
                        kt = kp * 2 + j
                        st.op("act", lambda e, g_t=g_t, j=j, kt=kt, v_t=v_t: e.activation(
                            out=v_t[:, kt, :], in_=g_t[:, j * 256:(j + 1) * 256], func=AF.Identity, scale=rkc[:, kt:kt + 1]),
                            reads=[g_b, b_rkc], writes=[v_b])
            wk, b_wk = wks.get(hd)
            if hd + 1 < 64:
                wks.load(hd + 1)
            wk3 = wk[:].rearrange("p (c n) -> p c n", n=128)
            k_t, k_b = KT.next()
            for k8 in range(8):
                g_t, g_b = pg.next()
                for c in range(4):
                    st.op("pe", lambda e, g_t=g_t, k8=k8, c=c, wk3=wk3: e.matmul(
                        g_t[:], lhsT=wk3[:, c, :], rhs=ckv[:, c, k8 * 512:(k8 + 1) * 512], start=(c == 0), stop=(c == 3)),
                        reads=[b_ckv, b_wk], writes=[g_b])
                st.op("dve", lambda e, g_t=g_t, k8=k8, k_t=k_t: e.tensor_tensor(
                    out=k_t[:, k8 * 512:(k8 + 1) * 512], in0=g_t[:], in1=rkv[:, k8 * 512:(k8 + 1) * 512], op=ALU.mult),
                    reads=[g_b, b_rkv], writes=[k_b])
            wq, b_wq = wqs.get(hd)
            if hd + 1 < 64:
                wqs.load(hd + 1)
            wq3 = wq[:].rearrange("p (c n) -> p c n", n=192)
            q_t, q_b = QT.next()
            qa_t, qa_b = qa[hd % 2], b_qa[hd % 2]
            for h2 in range(2):
                g_t, g_b = pg.next()
                for c in range(12):
                    st.op("pe", lambda e, g_t=g_t, h2=h2, c=c, wq3=wq3: e.matmul(
                        g_t[:], lhsT=wq3[:, c, 0:128], rhs=cq[:, c, h2 * 512:(h2 + 1) * 512], start=(c == 0), stop=(c == 11)),
                        reads=[b_cq, b_wq], writes=[g_b])
                st.op("dve", lambda e, g_t=g_t, h2=h2, q_t=q_t: e.tensor_tensor(
                    out=q_t[:, h2 * 512:(h2 + 1) * 512], in0=g_t[:], in1=rqs[:, h2 * 512:(h2 + 1) * 512], op=ALU.mult),
                    reads=[g_b, b_rqs], writes=[q_b])
                g_t, g_b = pg.next()
                for c in range(12):
                    st.op("pe", lambda e, g_t=g_t, h2=h2, c=c, wq3=wq3: e.matmul(
                        g_t[0:64, :], lhsT=wq3[:, c, 128:192], rhs=cq[:, c, h2 * 512:(h2 + 1) * 512], start=(c == 0), stop=(c == 11)),
                        reads=[b_cq, b_wq], writes=[g_b])
                a_t, a_b = xq.next()
                s_t, s_b = xs.next()
                st.op("act", lambda e, a_t=a_t, g_t=g_t: e.activation(out=a_t[:], in_=g_t[0:64, :], func=AF.Copy), reads=[g_b], writes=[a_b])
                st.op("dve", lambda e, s_t=s_t, a_t=a_t: e.tensor_copy(out=s_t[0:32, :], in_=a_t[32:64, :]), reads=[a_b], writes=[s_b])
                st.op("dve", lambda e, s_t=s_t, a_t=a_t: e.tensor_copy(out=s_t[32:64, :], in_=a_t[0:32, :]), reads=[a_b], writes=[s_b])
                st.op("dve", lambda e, a_t=a_t, h2=h2: e.tensor_tensor(out=a_t[:], in0=a_t[:], in1=cs[:, h2 * 512:(h2 + 1) * 512], op=ALU.mult),
                      reads=[a_b, b_cs], writes=[a_b])
                st.op("dve", lambda e, s_t=s_t, h2=h2: e.tensor_tensor(out=s_t[:], in0=s_t[:], in1=sn[:, h2 * 512:(h2 + 1) * 512], op=ALU.mult),
                      reads=[s_b, b_sn], writes=[s_b])
                st.op("dve", lambda e, a_t=a_t, s_t=s_t, qa_t=qa_t, h2=h2: e.tensor_tensor(
                    out=qa_t[0:64, h2 * 512:(h2 + 1) * 512], in0=a_t[:], in1=s_t[:], op=ALU.add), reads=[a_b, s_b], writes=[qa_b])
            for h2 in range(2):
                o_t, o_b = pO.next()
                da_t, da_b = dA.next()
                db_t, db_b = dB.next()
                qs = slice(h2 * 512, (h2 + 1) * 512)

                def emit_S(kt, k_t=k_t, q_t=q_t, qa_t=qa_t, qs=qs, k_b=k_b, q_b=q_b, qa_b=qa_b):
                    s_t, s_b = pS.next()
                    st.op("pe", lambda e, s_t=s_t, kt=kt, k_t=k_t, q_t=q_t, qs=qs: e.matmul(
                        s_t[:], lhsT=k_t[:, kt * 128:(kt + 1) * 128], rhs=q_t[:, qs],
                        start=True, stop=False), reads=[k_b, q_b], writes=[s_b])
                    st.op("pe", lambda e, s_t=s_t, kt=kt, qa_t=qa_t, qs=qs: e.matmul(
                        s_t[:], lhsT=ka[:, kt * 128:(kt + 1) * 128], rhs=qa_t[:, qs],
                        start=False, stop=True), reads=[b_ka, qa_b], writes=[s_b])
                    p_t, p_b = pt.next()
                    st.op("act", lambda e, p_t=p_t, s_t=s_t: e.activation(out=p_t[:], in_=s_t[:], func=AF.Exp), reads=[s_b], writes=[p_b])
                    return p_t, p_b

                def emit_PV(kt, p_t, p_b, o_t=o_t, o_b=o_b, v_t=v_t, v_b=v_b, hl=hl, da_t=da_t, da_b=da_b, db_t=db_t, db_b=db_b):
                    st.op("pe", lambda e, kt=kt, p_t=p_t, o_t=o_t, v_t=v_t, hl=hl: e.matmul(
                        o_t[:], lhsT=v_t[:, kt, hl * 128:(hl + 1) * 128], rhs=p_t[:],
                        start=(kt == 0), stop=(kt == NKT - 1)), reads=[v_b, p_b], writes=[o_b])
                    d_t, d_b, eng = (da_t, da_b, "dve") if kt % 2 == 0 else (db_t, db_b, "pool")
                    if kt < 2:
                        st.op(eng, lambda e, d_t=d_t, p_t=p_t: e.tensor_copy(out=d_t[:], in_=p_t[:]), reads=[p_b], writes=[d_b])
                    else:
                        st.op(eng, lambda e, d_t=d_t, p_t=p_t: e.tensor_tensor(out=d_t[:], in0=d_t[:], in1=p_t[:], op=ALU.add),
                              reads=[p_b, d_b], writes=[d_b])

                prev = emit_S(0)
                for kt in range(1, NKT):
                    cur = emit_S(kt)
                    emit_PV(kt - 1, *prev)
                    prev = cur
                emit_PV(NKT - 1, *prev)
                st.op("dve", lambda e, da_t=da_t, db_t=db_t: e.tensor_tensor(out=da_t[:], in0=da_t[:], in1=db_t[:], op=ALU.add),
                      reads=[da_b, db_b], writes=[da_b])
                pd_t, pd_b = pD.next()
                st.op("pe", lambda e, pd_t=pd_t, da_t=da_t: e.matmul(pd_t[:], lhsT=ones[:], rhs=da_t[:], start=True, stop=True),
                      reads=[b_ones, da_b], writes=[pd_b])
                r_t, r_b = rb.next()
                st.op("dve", lambda e, r_t=r_t, pd_t=pd_t: e.reciprocal(out=r_t[:], in_=pd_t[:]), reads=[pd_b], writes=[r_b])
                oo_t, oo_b = ot.next()
                st.op("dve", lambda e, oo_t=oo_t, o_t=o_t, r_t=r_t: e.tensor_tensor(out=oo_t[:], in0=o_t[:], in1=r_t[:], op=ALU.mult),
                      reads=[o_b, r_b], writes=[oo_b])
                st.dma("sp", oT_out[hd * 128:(hd + 1) * 128, qs], oo_t[:], reads=[oo_b])
        st.emit()


def stage_mod(nc, name, cT_d, aw_d, ab_d, mod_out):
    with contextlib.ExitStack() as es:
        st = Stage(nc, name)
        c = sb(nc, es, "c", [128, KC, 2], F32); b_c = Buf()
        st.dma("sp", c[:], cT_d.rearrange("p (k b) -> p k b", b=2), writes=[b_c])
        st.op("act", lambda e: e.activation(out=c[:], in_=c[:], func=AF.Silu), reads=[b_c], writes=[b_c])
        ab = sb(nc, es, "ab", [2, 24 * 256], F32); b_ab = Buf()
        st.dma("sp", ab[:], ab_d, writes=[b_ab])
        res = sb(nc, es, "res", [2, 24 * 256], F32); b_res = Buf()
        wst = Ring(nc, es, "aw", [128, KC * 256], F32, 3)
        pp = Ring(nc, es, "pp", [2, 256], F32, 2, psum=True)
        for blk in range(24):
            w_t, w_b = wst.next()
            st.dma("sp", w_t[:], aw_d[blk], writes=[w_b])
            w3 = w_t[:].rearrange("p (k n) -> p k n", n=256)
            p_t, p_b = pp.next()
            for kc in range(KC):
                st.op("pe", lambda e, p_t=p_t, kc=kc, w3=w3: e.matmul(p_t[:], lhsT=c[:, kc, :], rhs=w3[:, kc, :],
                                                                     start=(kc == 0), stop=(kc == KC - 1)),
                      reads=[b_c, w_b], writes=[p_b])
            st.op("dve", lambda e, p_t=p_t, blk=blk: e.tensor_tensor(
                out=res[:, blk * 256:(blk + 1) * 256], in0=p_t[:], in1=ab[:, blk * 256:(blk + 1) * 256], op=ALU.add),
                reads=[p_b, b_ab], writes=[b_res])
        st.dma("sp", mod_out, res[:], reads=[b_res])
        st.emit()


import numpy as np, ml_dtypes
BF = ml_dtypes.bfloat16
def tile_F(W, nb):
    K, N = W.shape; KC = K // 128
    return np.ascontiguousarray(W.reshape(KC, 128, N // nb, nb).transpose(2, 1, 0, 3)).reshape(N // nb, 128, KC * nb)
def cols(v):
    return np.ascontiguousarray(v.reshape(-1, 128).T)
def tile_out(W, P, CP, SBk):
    K = W.shape[0]; NSUB = CP // SBk
    a = W.reshape(P, NSUB, SBk, 128, 8, 512)
    a = a.transpose(0, 4, 1, 3, 2, 5)
    return np.ascontiguousarray(a).reshape(P * 8 * NSUB, 128, SBk * 512)
def tile_up(W):
    a = W.reshape(32, 128, 2, 84, 128)
    a = a.transpose(3, 2, 1, 0, 4)
    return np.ascontiguousarray(a).reshape(168, 128, 32 * 128)
def unit_cols(v):
    a = v.reshape(2, 84, 128).transpose(2, 1, 0)
    return np.ascontiguousarray(a).reshape(128, 168)
NEG = -30000.0
def rope_consts():
    invf = (10000.0 ** (-np.arange(0, 64, 2, dtype=np.float32) / 64)).astype(np.float32)
    invf2 = np.concatenate([invf, invf])[:, None].astype(np.float32)
    sgn = np.concatenate([-np.ones(32), np.ones(32)])[:, None].astype(np.float32)
    return invf2, sgn
def mask_A():
    k = np.arange(4096)
    return (k[None, :] // 64 == np.arange(64)[:, None]).astype(np.float32).astype(BF)
def mask_B(off):
    qc = (off + np.arange(1024)) // 64
    return np.where(np.arange(64)[:, None] > qc[None, :], NEG, 0.0).astype(np.float32).astype(BF)
def tile_heads_q(wuq):
    a = wuq.reshape(12, 128, 64, 192).transpose(2, 1, 0, 3)
    return np.ascontiguousarray(a).reshape(64, 128, 12 * 192)
def tile_heads_kv(wukv):
    a = wukv.reshape(4, 128, 64, 256)
    k = np.ascontiguousarray(a[..., :128].transpose(2, 1, 0, 3)).reshape(64, 128, 512)
    v = a[..., 128:].reshape(4, 128, 32, 2, 128).transpose(2, 1, 0, 3, 4)
    return k, np.ascontiguousarray(v).reshape(32, 128, 1024)


from concourse.bass_utils import run_bass_kernel_spmd

NCORES = 8
FUSED = False


def _mk():
    nc = bass.Bass("TRN2", target_bir_lowering=False)
    ins = {}

    def din(name, arrs, dt=None):
        a0 = arrs[0] if isinstance(arrs, list) else arrs
        if dt is None:
            dt = BF16 if a0.dtype == BF else (I32 if a0.dtype == np.int32 else F32)
        ins[name] = arrs
        return nc.dram_tensor(name, list(a0.shape), dt, kind="ExternalInput").ap()

    def dout(name, shape, dt=F32):
        return nc.dram_tensor(name, shape, dt, kind="ExternalOutput").ap()

    def dint(name, shape, dt=F32):
        return nc.dram_tensor(name, shape, dt, kind="Internal").ap()

    return nc, ins, din, dout, dint


def _run(nc, ins):
    maps = []
    for k in range(NCORES):
        m = {}
        for n, a in ins.items():
            m[n] = a[k] if isinstance(a, list) else a
        maps.append(m)
    res = run_bass_kernel_spmd(nc, maps, core_ids=list(range(NCORES)))
    return res.results


def _bc(row, p=128):
    return np.ascontiguousarray(np.broadcast_to(np.asarray(row)[None, :], (p, row.shape[-1])))


def kernel(x, c, positions, ada_w, ada_b, sgu_w_in, sgu_b_in, sgu_ln_g, sgu_ln_b, sgu_w_s, sgu_b_s, sgu_w_out,
           mla_w_in, mla_g_q, mla_g_kv, mla_w_uq, mla_w_ukv, mla_w_o, ffn_w_up, ffn_conv_w, ffn_conv_b, ffn_w_down, norm_g):
    f32 = np.float32
    x = np.asarray(x, f32); c = np.asarray(c, f32); positions = np.asarray(positions, np.int32)
    cores = [(k // 4, (k % 4) * 1024) for k in range(NCORES)]
    nc, ins, din, dout, dint = _mk()
    cT = np.ascontiguousarray(c.T.reshape(32, 128, 2).transpose(1, 0, 2)).reshape(128, 64)
    aw, ab = [], []
    for k in range(NCORES):
        sl = slice(k * 3072, (k + 1) * 3072)
        aw.append(np.concatenate([tile_F(np.asarray(ada_w[l][:, sl], f32), 256) for l in range(2)], axis=0))
        ab.append(_bc(np.concatenate([np.asarray(ada_b[l][sl], f32) for l in range(2)]), 2))
    stage_mod(nc, "mod", din("cT", cT), din("aw", aw), din("ab", ab), dout("modo", [2, 6144]))
    r = _run(nc, ins)
    del aw
    mod = np.zeros((2, 2, 24576), f32)
    for k in range(NCORES):
        for l in range(2):
            mod[l, :, k * 3072:(k + 1) * 3072] = r[k]["modo"][:, l * 3072:(l + 1) * 3072]
    mc = [[[cols(mod[l, b, j * 4096:(j + 1) * 4096]) for j in range(6)] for b in range(2)] for l in range(2)]

    def per_core(l, j):
        return [mc[l][b][j] for (b, _) in cores]

    xT = [np.ascontiguousarray(x[b, off:off + 1024].T) for (b, off) in cores]
    invf2, sgn = rope_consts()
    pos_bc = [_bc(positions[b, off:off + 1024], 64).astype(np.int32) for (b, off) in cores]

    def halos(hf):
        out = []
        for k, (b, off) in enumerate(cores):
            if off == 0:
                out.append(np.zeros((4096, 2), BF))
            else:
                out.append(np.ascontiguousarray(hf[k - 1][:, 1022:1024]))
        return out

    nc, ins, din, dout, dint = _mk()
    w_in = np.asarray(sgu_w_in[0], f32); b_in = np.asarray(sgu_b_in[0], f32)
    d_xT = din("xT", xT)
    hT = dint("hT", [4096, 1024], BF16); v = dint("v", [1024, 12288], BF16); stt = dint("stt", [128, 2, 8])
    zT = dint("zT", [12288, 1024], BF16)
    xT1 = dout("xT1", [4096, 1024]); hfT = dout("hfT", [4096, 1024], BF16)
    stage_norm(nc, "n0", d_xT, hT, din("scm", per_core(0, 1)), din("shm", per_core(0, 0)), BF16, True)
    stage_sgu1(nc, "s1", hT, din("wv", tile_F(w_in[:, 12288:], 256)), din("bv", b_in[None, 12288:]), v, stt)
    stage_sgu2(nc, "s2", hT, v, stt, din("wu", tile_F(w_in[:, :12288], 128)), din("bu", cols(b_in[:12288])),
               din("lng", _bc(np.asarray(sgu_ln_g[0], f32))), din("lnb", _bc(np.asarray(sgu_ln_b[0], f32))),
               din("wsT", np.ascontiguousarray(np.asarray(sgu_w_s[0], f32).transpose(2, 0, 1)).reshape(128, 2048)),
               din("bs", np.asarray(sgu_b_s[0], f32).reshape(1, 2048)), zT)
    stage_outproj(nc, "o1", zT, 12288, din("wo", tile_out(np.asarray(sgu_w_out[0], f32), 3, 32, 8)),
                  din("gm", per_core(0, 2)), d_xT, xT1, 3, 32, 8)
    stage_norm(nc, "n1", xT1, hfT, din("scf", per_core(0, 4)), din("shf", per_core(0, 3)), BF16, True)
    r = _run(nc, ins)
    xT = [r[k]["xT1"] for k in range(NCORES)]
    hf = [r[k]["hfT"] for k in range(NCORES)]
    del r, ins

    def ffn_stages(nc, din, dint, l, d_hf, d_halo, d_xin, xout):
        z2 = dint("z2", [10752, 1024], BF16)
        cwv = np.asarray(ffn_conv_w[l], f32)
        stage_ffnup(nc, "fu", d_hf, d_halo, din("wup", tile_up(np.asarray(ffn_w_up[l], f32))),
                    din("cw", np.concatenate([unit_cols(cwv[t]) for t in range(3)], axis=1)),
                    din("cb", unit_cols(np.asarray(ffn_conv_b[l], f32))), z2)
        stage_outproj(nc, "o2", z2, 10752, din("wdn", tile_out(np.asarray(ffn_w_down[l], f32), 3, 28, 7)),
                      din("gf", per_core(l, 5)), d_xin, xout, 3, 28, 7)

    nc, ins, din, dout, dint = _mk()
    d_x1 = din("xT1", xT)
    xT2 = dout("xT2", [4096, 1024])
    ffn_stages(nc, din, dint, 0, din("hfT", hf), din("halo", halos(hf)), d_x1, xT2)
    hT = dint("hT", [4096, 1024], BF16); cosk = dint("cosk", [64, 1024]); sink = dint("sink", [64, 1024])
    cqg = dout("cqg", [1536, 1024], BF16); ckvg = dout("ckvg", [512, 1024], BF16); kro = dout("kro", [64, 1024], BF16)
    rq = dout("rq", [1, 1024]); rkv = dout("rkv", [1, 1024])
    stage_norm(nc, "n2", xT2, hT, din("scm", per_core(1, 1)), din("shm", per_core(1, 0)), BF16, True)
    d_pos = din("pos", pos_bc, I32); d_iv = din("invf", invf2); d_sg = din("sgn", sgn)
    stage_rope(nc, "rk", d_pos, d_iv, d_sg, None, cosk, sink)
    wpad = np.zeros((4096, 17 * 128), f32); wpad[:, :2112] = np.asarray(mla_w_in[0], f32)
    stage_mlaproj(nc, "mp", hT, din("win", tile_F(wpad, 128)), din("gq", cols(np.asarray(mla_g_q[0], f32))),
                  din("gkv", cols(np.asarray(mla_g_kv[0], f32))), cosk, sink, cqg, ckvg, kro, rq, rkv)
    r = _run(nc, ins)
    xT = [r[k]["xT2"] for k in range(NCORES)]
    scale = f32(192 ** -0.5)
    A = mask_A()
    cq_l = [r[k]["cqg"] for k in range(NCORES)]
    rqs_row = [r[k]["rq"][0] * scale for k in range(NCORES)]
    ckv_all, ka_all, rkv_all = [], [], []
    for b in range(2):
        ks = [4 * b + j for j in range(4)]
        ckv_all.append(np.ascontiguousarray(np.concatenate([r[k]["ckvg"] for k in ks], axis=1)))
        ka = np.zeros((128, 4096), BF)
        ka[:64] = np.concatenate([r[k]["kro"] for k in ks], axis=1)
        ka[64:] = A
        ka_all.append(ka)
        rkv_all.append(np.concatenate([r[k]["rkv"][0] for k in ks]))
    del r, ins
    nc, ins, din, dout, dint = _mk()
    cosq = dint("cosq", [64, 1024]); sinq = dint("sinq", [64, 1024]); oT = dint("oT", [8192, 1024], BF16)
    xT3 = dout("xT3", [4096, 1024]); hfT = dout("hfT", [4096, 1024], BF16)
    stage_rope(nc, "rq", din("pos", pos_bc, I32), din("invf", invf2), din("sgn", sgn),
               din("mulq", [_bc(rqs_row[k], 64) for k in range(NCORES)]), cosq, sinq)
    wkh, wvh = tile_heads_kv(np.asarray(mla_w_ukv[0], f32))
    stage_attn(nc, "at", din("cqg", cq_l, BF16), din("rqs", [_bc(rqs_row[k]) for k in range(NCORES)]),
               din("ckvg", [ckv_all[b] for (b, _) in cores], BF16), din("rkvb", [_bc(rkv_all[b]) for (b, _) in cores]),
               din("rkc", [cols(rkv_all[b]) for (b, _) in cores]), din("ka", [ka_all[b] for (b, _) in cores], BF16),
               din("qb", [mask_B(off) for (_, off) in cores], BF16), cosq, sinq,
               din("wq", tile_heads_q(np.asarray(mla_w_uq[0], f32))), din("wk", wkh), din("wv", wvh), oT)
    d_x2 = din("xT2", xT)
    stage_outproj(nc, "o3", oT, 8192, din("wo", tile_out(np.asarray(mla_w_o[0], f32), 2, 32, 8)),
                  din("gm", per_core(1, 2)), d_x2, xT3, 2, 32, 8)
    stage_norm(nc, "n3", xT3, hfT, din("scf", per_core(1, 4)), din("shf", per_core(1, 3)), BF16, True)
    r = _run(nc, ins)
    xT = [r[k]["xT3"] for k in range(NCORES)]
    hf = [r[k]["hfT"] for k in range(NCORES)]
    del r, ins
    nc, ins, din, dout, dint = _mk()
    d_x3 = din("xT3", xT)
    xT4 = dint("xT4", [4096, 1024])
    ffn_stages(nc, din, dint, 1, din("hfT", hf), din("halo", halos(hf)), d_x3, xT4)
    outT = dout("outT", [4096, 1024])
    stage_norm(nc, "n4", xT4, outT, din("ng", cols(np.asarray(norm_g, f32))), None, F32, False)
    r = _run(nc, ins)
    out = np.zeros((2, 4096, 4096), f32)
    for k, (b, off) in enumerate(cores):
        out[b, off:off + 1024, :] = r[k]["outT"].T
    return out
```
